# Optimizing a Trainium2 kernel written in Bass

```python
import math
import jax, jax.numpy as jnp
from jax import lax
import numpy as np

D_MODEL = 2048
BATCH = 4
SEQ = 4096
DEPTH = 4

N_MIXERS = 2
N_META = 16
FNET_GROUPS = 8
SSD_EXPAND = 2
D_INNER = SSD_EXPAND * D_MODEL
SSD_HEAD_DIM = 64
SSD_HEADS = D_INNER // SSD_HEAD_DIM
SSD_GROUPS = 8
SSD_STATE = 128
CONV_WIDTH = 5
CHUNK = 256
GN = SSD_GROUPS * SSD_STATE
CONV_DIM = D_INNER + 2 * GN
D_IN_PROJ = D_INNER + CONV_DIM + 2 * SSD_HEADS
D_FF = (((8 * D_MODEL + 2) // 3 + 255) // 256) * 256
N_FNET = (DEPTH + 1) // 2
N_SSD = DEPTH // 2
EPS = 1e-6

kernel_name = "hybrid_fnet_ssd_meta_encoder"


def rms_norm(x, w):
    xf = x.astype(jnp.float32)
    y = xf * lax.rsqrt(jnp.mean(xf * xf, axis=-1, keepdims=True) + EPS)
    return (y * w.astype(jnp.float32)).astype(x.dtype)


def gated_group_rms_norm(y, z, w):
    b, L, dn = y.shape
    g = (y.astype(jnp.float32) * jax.nn.silu(z.astype(jnp.float32))).reshape(b, L, SSD_GROUPS, dn // SSD_GROUPS)
    g = g * lax.rsqrt(jnp.mean(g * g, axis=-1, keepdims=True) + EPS)
    return (g.reshape(b, L, dn) * w.astype(jnp.float32)).astype(z.dtype)


def fourier_mixer(u, w_out):
    b, L, d = u.shape
    ug = u.astype(jnp.float32).reshape(b, L, FNET_GROUPS, d // FNET_GROUPS)
    f = jnp.fft.fftn(ug, axes=(1, 3), norm="ortho").real.reshape(b, L, d)
    return f.astype(u.dtype) @ w_out


def centred_depthwise_conv(u, w, bias):
    k = w.shape[0]
    out = lax.conv_general_dilated(
        u, w.astype(u.dtype)[:, None, :], window_strides=(1,),
        padding=[(k // 2, k // 2)], dimension_numbers=("NWC", "WIO", "NWC"),
        feature_group_count=u.shape[-1])
    return out + bias.astype(u.dtype)


def segsum(a):
    t = a.shape[-1]
    cs = jnp.cumsum(a, axis=-1)
    diff = cs[..., :, None] - cs[..., None, :]
    mask = jnp.tril(jnp.ones((t, t), dtype=bool))
    return jnp.where(mask, diff, -jnp.inf)


def ssd_chunked(xdt, dA, Bm, Cm):
    b, lp, h, p = xdt.shape
    g, n = Bm.shape[2], Bm.shape[3]
    e = h // g
    c = lp // CHUNK
    X = xdt.reshape(b, c, CHUNK, g, e, p)
    Bc = Bm.reshape(b, c, CHUNK, g, n)
    Cc = Cm.reshape(b, c, CHUNK, g, n)
    A = dA.reshape(b, c, CHUNK, g, e).transpose(0, 3, 4, 1, 2)
    A_cs = jnp.cumsum(A, axis=-1)
    CB = jnp.einsum("bclgn,bcsgn->bgcls", Cc, Bc)
    M = jnp.exp(segsum(A)) * CB[:, :, None]
    y_diag = jnp.einsum("bgecls,bcsgep->bclgep", M, X)
    decay_states = jnp.exp(A_cs[..., -1:] - A_cs)
    states = jnp.einsum("bclgn,bgecl,bclgep->bcgepn", Bc, decay_states, X)
    states = jnp.concatenate([jnp.zeros_like(states[:, :1]), states], axis=1)
    a_last = jnp.pad(A_cs[..., -1], ((0, 0), (0, 0), (0, 0), (1, 0)))
    decay_chunk = jnp.exp(segsum(a_last))
    new_states = jnp.einsum("bgezc,bcgepn->bzgepn", decay_chunk, states)
    prev_states = new_states[:, :-1]
    y_off = jnp.einsum("bclgn,bcgepn,bgecl->bclgep", Cc, prev_states, jnp.exp(A_cs))
    return (y_diag + y_off).reshape(b, lp, h, p)


def bidir_ssd_mixer(u, w_in, conv_w, conv_b, dt_bias, a_log, d_skip, norm_w, w_out):
    b, L, _ = u.shape
    proj = u @ w_in
    z = proj[..., :D_INNER]
    xbc = proj[..., D_INNER:D_INNER + CONV_DIM]
    dt_raw = proj[..., D_INNER + CONV_DIM:].reshape(b, L, 2, SSD_HEADS)
    xbc = jax.nn.silu(centred_depthwise_conv(xbc, conv_w, conv_b)).astype(jnp.float32)
    xs = xbc[..., :D_INNER].reshape(b, L, SSD_HEADS, SSD_HEAD_DIM)
    Bm = xbc[..., D_INNER:D_INNER + GN].reshape(b, L, SSD_GROUPS, SSD_STATE)
    Cm = xbc[..., D_INNER + GN:].reshape(b, L, SSD_GROUPS, SSD_STATE)
    dt = jax.nn.softplus(dt_raw.astype(jnp.float32) + dt_bias.astype(jnp.float32))
    A = -jnp.exp(a_log.astype(jnp.float32))
    front = CHUNK - N_META
    tail = (-(L - N_META)) % CHUNK
    pad = lambda t: jnp.pad(t, [(0, 0), (front, tail)] + [(0, 0)] * (t.ndim - 2))
    rev = lambda t: jnp.flip(t, axis=1)
    xs_p, B_p, C_p, dt_p = pad(xs), pad(Bm), pad(Cm), pad(dt)
    dt_f, dt_b = dt_p[:, :, 0], dt_p[:, :, 1]
    y_f = ssd_chunked(xs_p * dt_f[..., None], dt_f * A[0], B_p, C_p)
    y_b = rev(ssd_chunked(rev(xs_p * dt_b[..., None]), rev(dt_b * A[1]), rev(B_p), rev(C_p)))
    y = (y_f + y_b)[:, front:front + L] + xs * d_skip.astype(jnp.float32)[:, None]
    y = y.reshape(b, L, D_INNER)
    return gated_group_rms_norm(y, z, norm_w) @ w_out


def swiglu_ffn(u, w_gate, w_up, w_down):
    return (jax.nn.silu(u @ w_gate) * (u @ w_up)) @ w_down


def setup_inputs(seed: int = 0) -> dict:
    key = jax.random.key(seed)
    ks = jax.random.split(key, 18)
    f32 = jnp.float32
    nrm = lambda k, shape, s: jax.random.normal(k, shape, f32) * s
    dt0 = jnp.exp(jax.random.uniform(ks[9], (N_SSD, 2, SSD_HEADS), f32, math.log(1e-3), math.log(1e-1)))
    return {
        "x": nrm(ks[0], (BATCH, SEQ, D_MODEL), 1.0),
        "meta_tokens": nrm(ks[1], (N_META, D_MODEL), 1.0),
        "norm_mix_w": 1.0 + nrm(ks[2], (DEPTH, D_MODEL), 0.02),
        "norm_ffn_w": 1.0 + nrm(ks[3], (DEPTH, D_MODEL), 0.02),
        "norm_final_w": 1.0 + nrm(ks[4], (D_MODEL,), 0.02),
        "fnet_w_out": nrm(ks[5], (N_FNET, D_MODEL, D_MODEL), D_MODEL ** -0.5),
        "ssd_w_in": nrm(ks[6], (N_SSD, D_MODEL, D_IN_PROJ), D_MODEL ** -0.5),
        "ssd_conv_w": nrm(ks[7], (N_SSD, CONV_WIDTH, CONV_DIM), CONV_WIDTH ** -0.5),
        "ssd_conv_b": nrm(ks[8], (N_SSD, CONV_DIM), 0.01),
        "ssd_dt_bias": dt0 + jnp.log(-jnp.expm1(-dt0)),
        "ssd_a_log": jnp.log(jax.random.uniform(ks[10], (N_SSD, 2, SSD_HEADS), f32, 1.0, 16.0)),
        "ssd_d": 1.0 + nrm(ks[11], (N_SSD, SSD_HEADS), 0.02),
        "ssd_norm_w": 1.0 + nrm(ks[12], (N_SSD, D_INNER), 0.02),
        "ssd_w_out": nrm(ks[13], (N_SSD, D_INNER, D_MODEL), D_INNER ** -0.5),
        "ffn_w_gate": nrm(ks[14], (DEPTH, D_MODEL, D_FF), D_MODEL ** -0.5),
        "ffn_w_up": nrm(ks[15], (DEPTH, D_MODEL, D_FF), D_MODEL ** -0.5),
        "ffn_w_down": nrm(ks[16], (DEPTH, D_FF, D_MODEL), D_FF ** -0.5),
    }


def reference(x, meta_tokens, norm_mix_w, norm_ffn_w, norm_final_w, fnet_w_out,
              ssd_w_in, ssd_conv_w, ssd_conv_b, ssd_dt_bias, ssd_a_log, ssd_d,
              ssd_norm_w, ssd_w_out, ffn_w_gate, ffn_w_up, ffn_w_down):
    b = x.shape[0]
    meta = jnp.broadcast_to(meta_tokens.astype(x.dtype)[None], (b, N_META, x.shape[-1]))
    h = jnp.concatenate([meta, x], axis=1)
    for i in range(DEPTH):
        j = i // N_MIXERS
        u = rms_norm(h, norm_mix_w[i])
        if i % N_MIXERS == 0:
            h = h + fourier_mixer(u, fnet_w_out[j])
        else:
            h = h + bidir_ssd_mixer(u, ssd_w_in[j], ssd_conv_w[j], ssd_conv_b[j],
                                    ssd_dt_bias[j], ssd_a_log[j], ssd_d[j],
                                    ssd_norm_w[j], ssd_w_out[j])
        h = h + swiglu_ffn(rms_norm(h, norm_ffn_w[i]), ffn_w_gate[i], ffn_w_up[i], ffn_w_down[i])
    return rms_norm(h, norm_final_w)[:, N_META:]
```

```python
import contextlib, math, os
import numpy as np
import ml_dtypes
import concourse.bass as bass
import concourse.mybir as mybir
from concourse.bass_utils import run_bass_kernel_spmd

F32 = mybir.dt.float32
BF16 = mybir.dt.bfloat16
AF = mybir.ActivationFunctionType
ALU = mybir.AluOpType

D = 2048
NMETA = 16
DI = 4096
NH = 64
HP = 64
NG = 8
NS = 128
GN = 1024
CONVD = DI + 2 * GN
DPROJ = DI + CONVD + 2 * NH
DFF = 5632
EPS = 1e-6
KW = 5


def tiles(n, sz):
    return [(i, min(sz, n - i)) for i in range(0, n, sz)]


class Res:
    __slots__ = ("w", "r")

    def __init__(self):
        self.w = None
        self.r = {}


class Prog:
    ENG = ("pe", "act", "dve", "pool", "sp")
    METH = {"pe": "tensor", "act": "scalar", "dve": "vector", "pool": "gpsimd", "sp": "sync"}

    def __init__(self, nc, es):
        self.nc = nc
        self.semobj = {}
        self.ccnt = {}
        for e in ("pe", "act", "dve", "pool"):
            self.semobj[("c", e)] = es.enter_context(nc.semaphore("c_" + e))
            self.ccnt[e] = 0
        self.dn = {"sp": 16, "pool": 12, "act": 6}
        self.dval = {}
        self.dnext = {}
        for e, n in self.dn.items():
            for i in range(n):
                self.semobj[("d", e, i)] = es.enter_context(nc.semaphore("d_%s%d" % (e, i)))
            self.dval[e] = [0] * n
            self.dnext[e] = 0
        self.known = {e: {} for e in self.ENG}
        self.q = {e: [] for e in self.ENG}
        self.epoch = 0

    def _wait(self, eng, tok):
        key, val, ep = tok
        if ep != self.epoch:
            return
        if self.known[eng].get(key, 0) >= val:
            return
        self.known[eng][key] = val
        sem = self.semobj[key]
        self.q[eng].append(lambda e: e.wait_ge(sem, val))

    def _sync(self, eng, reads, writes):
        own = ("c", eng)
        for r in reads:
            if r.w is not None:
                if not (eng == "pe" and r.w[0] == own):
                    self._wait(eng, r.w)
        for w in writes:
            if w.w is not None:
                if not (eng == "pe" and w.w[0] == own):
                    self._wait(eng, w.w)
            for key, (val, ep) in w.r.items():
                if eng == "pe" and key == own:
                    continue
                self._wait(eng, (key, val, ep))

    def _mark(self, tok, reads, writes):
        for r in reads:
            r.r[tok[0]] = (tok[1], tok[2])
        for w in writes:
            w.w = tok
            w.r = {}

    def op(self, eng, fn, reads=(), writes=()):
        self._sync(eng, reads, writes)
        self.ccnt[eng] += 1
        v = self.ccnt[eng]
        sem = self.semobj[("c", eng)]
        self.q[eng].append(lambda e: fn(e).then_inc(sem, 1))
        self._mark((("c", eng), v, self.epoch), reads, writes)

    def dma(self, eng, out, in_, reads=(), writes=()):
        n = self.dn[eng]
        slot = self.dnext[eng] % n
        self.dnext[eng] += 1
        key = ("d", eng, slot)
        if self.dval[eng][slot] > 0:
            self._wait(eng, (key, self.dval[eng][slot], self.epoch))
        self._sync(eng, reads, writes)
        self.dval[eng][slot] += 16
        v = self.dval[eng][slot]
        sem = self.semobj[key]
        self.q[eng].append(lambda e: e.dma_start(out=out, in_=in_).then_inc(sem, 16))
        self._mark((key, v, self.epoch), reads, writes)

    def flush(self):
        for eng, n in self.dn.items():
            for slot in range(n):
                if self.dval[eng][slot] > 0:
                    self._wait(eng, (("d", eng, slot), self.dval[eng][slot], self.epoch))
        with self.nc.Block() as blk:
            for eng in self.ENG:
                lst = self.q[eng]

                def body(e, lst=lst):
                    for f in lst:
                        f(e)

                getattr(blk, self.METH[eng])(body)
        self.q = {e: [] for e in self.ENG}
        self.epoch += 1


class Ring:
    def __init__(self, es, nc, name, n, shape, dtype, psum=False):
        self.t = []
        for i in range(n):
            if psum:
                t = es.enter_context(nc.psum_tensor(_uname(name), shape, dtype))
            else:
                t = es.enter_context(nc.sbuf_tensor(_uname(name), shape, dtype))
            self.t.append((t, Res()))
        self.i = 0

    def next(self):
        x = self.t[self.i % len(self.t)]
        self.i += 1
        return x


_UID = [0]


def _uname(name):
    _UID[0] += 1
    return "%s_u%d" % (name, _UID[0])


def sb(es, nc, name, shape, dtype):
    return es.enter_context(nc.sbuf_tensor(_uname(name), shape, dtype))


def stage_copy_in(P, nc, res, x_in, T):
    for i, (r0, rn) in enumerate(tiles(D, 256)):
        P.dma("sp" if i % 2 == 0 else "act", res[r0:r0 + rn, :], x_in[r0:r0 + rn, :])
    P.flush()


def stage_zero_pad(P, nc, tens, T, Tp):
    if Tp == T:
        return
    with contextlib.ExitStack() as es:
        zf = sb(es, nc, "zf", [128, Tp - T], F32)
        zb = sb(es, nc, "zb", [128, Tp - T], BF16)
        rz = Res()
        P.op("pool", lambda e: e.memset(zf[:], 0.0), writes=[rz])
        P.op("pool", lambda e: e.memset(zb[:], 0.0), writes=[rz])
        k = 0
        for t in tens:
            rows = t.shape[0]
            src = zb if t.dtype == BF16 else zf
            for r0 in range(0, rows, 128):
                P.dma("sp" if k % 2 == 0 else "act", t[r0:r0 + 128, T:Tp], src[:], reads=[rz])
                k += 1
        P.flush()


def stage_rmsnorm(P, nc, res, w_sb, wcol0, out, T, consts, out_f32=False):
    KC = D // 128
    with contextlib.ExitStack() as es:
        onesD = sb(es, nc, "onesD", [128, 128], F32)
        r_ones = Res()
        P.op("pool", lambda e: e.memset(onesD[:], 1.0 / D), writes=[r_ones])
        xr = Ring(es, nc, "nx", 2, [128, KC, 512], F32)
        sq = Ring(es, nc, "nsq", 3, [128, 512], F32)
        rs = Ring(es, nc, "nrs", 2, [128, 512], F32)
        ob = Ring(es, nc, "nob", 4, [128, 512], F32 if out_f32 else BF16)
        ps = Ring(es, nc, "nps", 2, [128, 512], F32, psum=True)
        for (t0, tn) in tiles(T, 512):
            xt, xres = xr.next()
            for kc in range(KC):
                pass
            for j in range(4):
                P.dma("sp" if j % 2 == 0 else "act", xt[:, 4 * j:4 * j + 4, :tn],
                      res[512 * j:512 * j + 512, t0:t0 + tn].rearrange("(kc p) t -> p kc t", p=128),
                      writes=[xres])
            pt, pres = ps.next()
            for kc in range(KC):
                st, sres = sq.next()
                P.op("act", lambda e, st=st, xt=xt, kc=kc, tn=tn: e.activation(st[:, :tn], xt[:, kc, :tn], AF.Square),
                     reads=[xres], writes=[sres])
                P.op("pe", lambda e, pt=pt, st=st, kc=kc, tn=tn: e.matmul(pt[:, :tn], onesD[:], st[:, :tn], start=(kc == 0), stop=(kc == KC - 1)),
                     reads=[sres, r_ones], writes=[pres])
            rt, rres = rs.next()
            P.op("act", lambda e, rt=rt, pt=pt, tn=tn: e.activation(rt[:, :tn], pt[:, :tn], AF.Ln, bias=EPS, scale=1.0),
                 reads=[pres], writes=[rres])
            P.op("act", lambda e, rt=rt, tn=tn: e.activation(rt[:, :tn], rt[:, :tn], AF.Exp, scale=-0.5),
                 reads=[rres], writes=[rres])
            for kc in range(KC):
                ot, ores = ob.next()
                P.op("dve", lambda e, ot=ot, xt=xt, kc=kc, tn=tn, rt=rt: e.scalar_tensor_tensor(
                    ot[:, :tn], xt[:, kc, :tn], w_sb[:, wcol0 + kc:wcol0 + kc + 1], rt[:, :tn], ALU.mult, ALU.mult),
                    reads=[xres, rres], writes=[ores])
                P.dma("sp" if kc % 2 == 0 else "act", out[kc * 128:(kc + 1) * 128, t0:t0 + tn], ot[:, :tn], reads=[ores])
        P.flush()


def stage_gemm(P, nc, name, xT, K, T, wts, M, tgroup, epi_setup, epi):
    KC = K // 128
    MC = M // 128
    nw = len(wts)
    with contextlib.ExitStack() as es:
        xs = sb(es, nc, name + "_x", [128, KC, tgroup], BF16)
        xres = [Res() for _ in range(KC)]
        wr = Ring(es, nc, name + "_w", 3 * nw, [128, KC * 128], BF16)
        ps = Ring(es, nc, name + "_ps", (8 // nw // 2) * nw if nw > 1 else 4, [128, 512], F32, psum=True)
        ctx = epi_setup(es)
        for (g0, gn) in tiles(T, tgroup):
            for kc in range(KC):
                P.dma("sp" if kc % 2 == 0 else "act", xs[:, kc, :gn], xT[kc * 128:(kc + 1) * 128, g0:g0 + gn],
                      writes=[xres[kc]])
            for mc in range(MC):
                wb = []
                for j in range(nw):
                    wt, wres = wr.next()
                    P.dma("pool", wt[:], wts[j][mc], writes=[wres])
                    wb.append((wt, wres))
                for (t0, tn) in tiles(gn, 512):
                    pl = []
                    for j in range(nw):
                        pt, pres = ps.next()
                        wt, wres = wb[j]
                        for kc in range(KC):
                            P.op("pe", lambda e, pt=pt, wt=wt, kc=kc, t0=t0, tn=tn: e.matmul(
                                pt[:, :tn], wt[:, kc * 128:(kc + 1) * 128], xs[:, kc, t0:t0 + tn],
                                start=(kc == 0), stop=(kc == KC - 1)),
                                reads=[wres, xres[kc]], writes=[pres])
                        pl.append((pt, pres))
                    epi(ctx, mc, g0 + t0, tn, pl)
        P.flush()


def epi_resadd(P, nc, res):
    def setup(es):
        return dict(rb=Ring(es, nc, "ra_r", 3, [128, 512], F32), ob=Ring(es, nc, "ra_o", 3, [128, 512], F32), k=[0])

    def epi(c, mc, t0, tn, pl):
        pt, pres = pl[0]
        rt, rres = c["rb"].next()
        ot, ores = c["ob"].next()
        c["k"][0] += 1
        q = "sp" if c["k"][0] % 2 == 0 else "act"
        dst = res[mc * 128:(mc + 1) * 128, t0:t0 + tn]
        P.dma(q, rt[:, :tn], dst, writes=[rres])
        P.op("dve", lambda e: e.tensor_tensor(ot[:, :tn], pt[:, :tn], rt[:, :tn], ALU.add), reads=[pres, rres], writes=[ores])
        P.dma(q, dst, ot[:, :tn], reads=[ores])
    return setup, epi


def epi_store(P, nc, out, dtype):
    def setup(es):
        return dict(ob=Ring(es, nc, "st_o", 4, [128, 512], dtype), k=[0])

    def epi(c, mc, t0, tn, pl):
        pt, pres = pl[0]
        ot, ores = c["ob"].next()
        c["k"][0] += 1
        if c["k"][0] % 2 == 0:
            P.op("dve", lambda e: e.tensor_copy(ot[:, :tn], pt[:, :tn]), reads=[pres], writes=[ores])
        else:
            P.op("act", lambda e: e.activation(ot[:, :tn], pt[:, :tn], AF.Copy), reads=[pres], writes=[ores])
        P.dma("sp" if c["k"][0] % 2 == 0 else "act", out[mc * 128:(mc + 1) * 128, t0:t0 + tn], ot[:, :tn], reads=[ores])
    return setup, epi


def epi_swiglu(P, nc, hT):
    def setup(es):
        return dict(sg=Ring(es, nc, "sw_s", 3, [128, 512], F32), ob=Ring(es, nc, "sw_o", 4, [128, 512], BF16), k=[0])

    def epi(c, mc, t0, tn, pl):
        (pg, rg), (pu, ru) = pl
        st, sres = c["sg"].next()
        ot, ores = c["ob"].next()
        c["k"][0] += 1
        P.op("act", lambda e: e.activation(st[:, :tn], pg[:, :tn], AF.Silu), reads=[rg], writes=[sres])
        P.op("dve", lambda e: e.tensor_tensor(ot[:, :tn], st[:, :tn], pu[:, :tn], ALU.mult), reads=[sres, ru], writes=[ores])
        P.dma("sp" if c["k"][0] % 2 == 0 else "act", hT[mc * 128:(mc + 1) * 128, t0:t0 + tn], ot[:, :tn], reads=[ores])
    return setup, epi


def stage_fnet_chan(P, nc, uT, csc, pq, Tp):
    with contextlib.ExitStack() as es:
        cs_sb = sb(es, nc, "fc_cs", [128, 2, 512], BF16)
        rcs = Res()
        P.dma("sp", cs_sb[:], csc.rearrange("(kc p) c -> p kc c", p=128), writes=[rcs])
        ur = Ring(es, nc, "fc_u", 2, [128, 16, 512], BF16)
        ps = Ring(es, nc, "fc_ps", 4, [128, 512], F32, psum=True)
        ob = Ring(es, nc, "fc_o", 4, [128, 512], BF16)
        k = 0
        for (t0, tn) in tiles(Tp, 512):
            ut, ures = ur.next()
            for j in range(4):
                P.dma("sp" if j % 2 == 0 else "act", ut[:, 4 * j:4 * j + 4, :tn],
                      uT[512 * j:512 * j + 512, t0:t0 + tn].rearrange("(kc p) t -> p kc t", p=128), writes=[ures])
            for (s0, sn) in tiles(tn, 128):
                for g in range(8):
                    pt, pres = ps.next()
                    for h in range(2):
                        P.op("pe", lambda e, pt=pt, ut=ut, g=g, h=h, s0=s0: e.matmul(
                            pt[:, :], ut[:, 2 * g + h, s0:s0 + 128], cs_sb[:, h, :], start=(h == 0), stop=(h == 1)),
                            reads=[ures, rcs], writes=[pres])
                    ot, ores = ob.next()
                    k += 1
                    if k % 2 == 0:
                        P.op("dve", lambda e, ot=ot, pt=pt: e.tensor_copy(ot[:], pt[:]), reads=[pres], writes=[ores])
                    else:
                        P.op("act", lambda e, ot=ot, pt=pt: e.activation(ot[:], pt[:], AF.Copy), reads=[pres], writes=[ores])
                    P.dma("sp" if k % 2 == 0 else "act", pq[t0 + s0:t0 + s0 + 128, g * 512:(g + 1) * 512], ot[:], reads=[ores])
        P.flush()


def stage_fnet_pos(P, nc, pq, tabc, tabs, rT, T, Tp):
    NCH = Tp // 128
    scale = 1.0 / math.sqrt(T * 256.0)
    ktiles = tiles(T, 256)
    with contextlib.ExitStack() as es:
        pqs = sb(es, nc, "fp_pq", [128, NCH, 1024], BF16)
        pres_l = [Res() for _ in range(NCH)]
        tcr = Ring(es, nc, "fp_tc", 2, [128, NCH * 256], BF16)
        tsr = Ring(es, nc, "fp_ts", 2, [128, NCH * 256], BF16)
        ps = Ring(es, nc, "fp_ps", 4, [128, 512], F32, psum=True)
        ob = Ring(es, nc, "fp_o", 4, [128, 256], BF16)
        k = 0
        for cq in range(4):
            for n in range(NCH):
                P.dma("sp" if n % 2 == 0 else "act", pqs[:, n, :], pq[n * 128:(n + 1) * 128, cq * 1024:(cq + 1) * 1024],
                      writes=[pres_l[n]])
            for kt, (k0, kn) in enumerate(ktiles):
                tc, tcres = tcr.next()
                ts, tsres = tsr.next()
                P.dma("sp", tc[:], tabc[kt], writes=[tcres])
                P.dma("act", ts[:], tabs[kt], writes=[tsres])
                for cb in range(4):
                    g, h = cb // 2, cb % 2
                    pt, pres = ps.next()
                    for n in range(NCH):
                        P.op("pe", lambda e, pt=pt, n=n, g=g, h=h, tc=tc, kn=kn: e.matmul(
                            pt[:, :kn], pqs[:, n, g * 512 + h * 128:g * 512 + h * 128 + 128], tc[:, n * 256:n * 256 + kn],
                            start=(n == 0), stop=False), reads=[pres_l[n], tcres], writes=[pres])
                    for n in range(NCH):
                        P.op("pe", lambda e, pt=pt, n=n, g=g, h=h, ts=ts, kn=kn: e.matmul(
                            pt[:, :kn], pqs[:, n, g * 512 + 256 + h * 128:g * 512 + 256 + h * 128 + 128], ts[:, n * 256:n * 256 + kn],
                            start=False, stop=(n == NCH - 1)), reads=[pres_l[n], tsres], writes=[pres])
                    ot, ores = ob.next()
                    k += 1
                    P.op("act", lambda e, ot=ot, pt=pt, kn=kn: e.activation(ot[:, :kn], pt[:, :kn], AF.Copy, scale=scale),
                         reads=[pres], writes=[ores])
                    row = (cq * 4 + cb) * 128
                    P.dma("sp" if k % 2 == 0 else "act", rT[row:row + 128, k0:k0 + kn], ot[:, :kn], reads=[ores])
        P.flush()


def stage_conv(P, nc, proj, cw, cb, xbcT, xtok, btok, consts_sb, rconst, T, Tp):
    NCH = Tp // 128
    ident = consts_sb[:, 0, :]
    with contextlib.ExitStack() as es:
        cw_sb = sb(es, nc, "cv_w", [128, 48, KW], F32)
        cb_sb = sb(es, nc, "cv_b", [128, 48], F32)
        rw = Res()
        P.dma("sp", cw_sb[:], cw, writes=[rw])
        P.dma("sp", cb_sb[:], cb, writes=[rw])
        inr = Ring(es, nc, "cv_in", 2, [128, Tp + 4], F32)
        acc = Ring(es, nc, "cv_acc", 2, [128, Tp], F32)
        sil = sb(es, nc, "cv_sil", [128, 4, Tp], F32)
        silres = [Res() for _ in range(4)]
        ob = Ring(es, nc, "cv_ob", 2, [128, Tp], BF16)
        ps = Ring(es, nc, "cv_ps", 2, [128, 512], F32, psum=True)
        tb = Ring(es, nc, "cv_tb", 3, [128, 512], BF16)
        for blk in range(12):
            for j in range(4):
                ch = blk * 4 + j
                it, ires = inr.next()
                P.op("pool", lambda e, it=it: e.memset(it[:, 0:2], 0.0), writes=[ires])
                P.op("pool", lambda e, it=it: e.memset(it[:, 2 + T:Tp + 4], 0.0), writes=[ires])
                P.dma("sp" if j % 2 == 0 else "act", it[:, 2:2 + T], proj[DI + ch * 128:DI + (ch + 1) * 128, 0:T], writes=[ires])
                at, ares = acc.next()
                P.op("dve", lambda e, at=at, it=it, ch=ch: e.tensor_scalar(at[:, :T], it[:, 0:T], cw_sb[:, ch, 0:1], 0.0, ALU.mult, ALU.add),
                     reads=[ires, rw], writes=[ares])
                for kk in range(1, KW):
                    P.op("dve", lambda e, at=at, it=it, ch=ch, kk=kk: e.scalar_tensor_tensor(
                        at[:, :T], it[:, kk:kk + T], cw_sb[:, ch, kk:kk + 1], at[:, :T], ALU.mult, ALU.add),
                        reads=[ires, rw, ares], writes=[ares])
                if Tp > T:
                    P.op("pool", lambda e, j=j: e.memset(sil[:, j, T:Tp], 0.0), writes=[silres[j]])
                P.op("act", lambda e, at=at, ch=ch, j=j: e.activation(sil[:, j, :T], at[:, :T], AF.Silu, bias=cb_sb[:, ch:ch + 1], scale=1.0),
                     reads=[ares, rw], writes=[silres[j]])
                ot, ores = ob.next()
                P.op("pool", lambda e, ot=ot, j=j: e.tensor_copy(ot[:], sil[:, j, :]), reads=[silres[j]], writes=[ores])
                P.dma("sp" if j % 2 == 0 else "act", xbcT[ch * 128:(ch + 1) * 128, :], ot[:], reads=[ores])
            if blk < 10:
                for n in range(NCH):
                    pt, pres = ps.next()
                    for j in range(4):
                        P.op("pe", lambda e, pt=pt, j=j, n=n: e.transpose(pt[:, j * 128:(j + 1) * 128], sil[:, j, n * 128:(n + 1) * 128], ident),
                             reads=[silres[j], rconst], writes=[pres])
                    tt, tres = tb.next()
                    if n % 2 == 0:
                        P.op("dve", lambda e, tt=tt, pt=pt: e.tensor_copy(tt[:], pt[:]), reads=[pres], writes=[tres])
                    else:
                        P.op("act", lambda e, tt=tt, pt=pt: e.activation(tt[:], pt[:], AF.Copy), reads=[pres], writes=[tres])
                    if blk < 8:
                        dst = xtok[n * 128:(n + 1) * 128, blk * 512:(blk + 1) * 512]
                    else:
                        dst = btok[n * 128:(n + 1) * 128, (blk - 8) * 512:(blk - 7) * 512]
                    P.dma("sp" if n % 2 == 0 else "act", dst, tt[:], reads=[tres])
        P.flush()


def stage_dt(P, nc, proj, dtb, alog, aux, consts_sb, rconst, T, Tp):
    NCH = Tp // 128
    ident = consts_sb[:, 0, :]
    tri_f = consts_sb[:, 1, :]
    tri_b = consts_sb[:, 2, :]
    ones = consts_sb[:, 5, :]
    with contextlib.ExitStack() as es:
        pr = sb(es, nc, "dt_pr", [128, 64], F32)
        rp = Res()
        P.dma("sp", pr[:], dtb, writes=[rp])
        Acol = sb(es, nc, "dt_A", [128, 1], F32)
        rA = Res()
        P.op("act", lambda e: e.activation(Acol[:], pr[:, 1:2], AF.Exp), reads=[rp], writes=[rA])
        P.op("dve", lambda e: e.tensor_scalar(Acol[:], Acol[:], -1.0, 0.0, ALU.mult, ALU.add), reads=[rA], writes=[rA])
        x = sb(es, nc, "dt_x", [128, Tp], F32)
        ax = sb(es, nc, "dt_ax", [128, Tp], F32)
        dtT = sb(es, nc, "dt_dt", [128, Tp], F32)
        dAT = sb(es, nc, "dt_dA", [128, Tp], F32)
        rx, rax, rdt, rdA = Res(), Res(), Res(), Res()
        P.dma("sp", x[:, :T], proj[DI + CONVD:DI + CONVD + 128, 0:T], writes=[rx])
        P.op("dve", lambda e: e.tensor_scalar(x[:, :T], x[:, :T], pr[:, 0:1], 30.0, ALU.add, ALU.min), reads=[rx, rp], writes=[rx])
        P.op("act", lambda e: e.activation(ax[:, :T], x[:, :T], AF.Exp), reads=[rx], writes=[rax])
        if Tp > T:
            P.op("pool", lambda e: e.memset(dtT[:, T:Tp], 0.0), writes=[rdt])
        P.op("act", lambda e: e.activation(dtT[:, :T], ax[:, :T], AF.Ln, bias=1.0, scale=1.0), reads=[rax], writes=[rdt])
        P.op("dve", lambda e: e.tensor_scalar(dAT[:], dtT[:], Acol[:, 0:1], 0.0, ALU.mult, ALU.add), reads=[rdt, rA], writes=[rdA])
        ps = Ring(es, nc, "dt_ps", 5, [128, 512], F32, psum=True)
        tk = Ring(es, nc, "dt_tk", 2, [128, 5, 128], F32)
        tm = Ring(es, nc, "dt_tm", 2, [128, 2, 128], F32)
        KDT = int(os.environ.get("KDT", "9"))
        for n in range(NCH if KDT >= 1 else 0):
            sl = slice(n * 128, (n + 1) * 128)
            pt, pres = ps.next()
            at, ares = tk.next()
            mt, mres = tm.next()
            P.op("pe", lambda e, pt=pt, sl=sl: e.transpose(pt[:, 0:128], dtT[:, sl], ident), reads=[rdt, rconst], writes=[pres])
            P.op("pe", lambda e, pt=pt, sl=sl: e.transpose(pt[:, 128:256], dAT[:, sl], ident), reads=[rdA, rconst], writes=[pres])
            P.op("act", lambda e, pt=pt, at=at: e.activation(at[:, 0, :], pt[:, 0:128], AF.Copy), reads=[pres], writes=[ares])
            P.op("act", lambda e, pt=pt, at=at: e.activation(at[:, 1, :], pt[:, 128:256], AF.Copy), reads=[pres], writes=[ares])
            if KDT < 2:
                continue
            pa, rpa = ps.next()
            pb, rpb = ps.next()
            pc, rpc = ps.next()
            P.op("pe", lambda e, pa=pa, at=at: e.matmul(pa[:, 0:128], tri_f, at[:, 1, :], start=True, stop=True), reads=[ares, rconst], writes=[rpa])
            P.op("pe", lambda e, pb=pb, at=at: e.matmul(pb[:, 0:128], tri_b, at[:, 1, :], start=True, stop=True), reads=[ares, rconst], writes=[rpb])
            P.op("pe", lambda e, pc=pc, at=at: e.matmul(pc[:, 0:128], ones, at[:, 1, :], start=True, stop=True), reads=[ares, rconst], writes=[rpc])
            P.op("act", lambda e, pa=pa, mt=mt: e.activation(mt[:, 0, 0:64], pa[:, 0:64], AF.Copy), reads=[rpa], writes=[mres])
            P.op("act", lambda e, pb=pb, mt=mt: e.activation(mt[:, 0, 64:128], pb[:, 64:128], AF.Copy), reads=[rpb], writes=[mres])
            P.op("act", lambda e, pc=pc, mt=mt: e.activation(mt[:, 1, :], pc[:, 0:128], AF.Copy), reads=[rpc], writes=[mres])
            P.op("act", lambda e, mt=mt, at=at: e.activation(at[:, 2, :], mt[:, 0, :], AF.Exp), reads=[mres], writes=[ares])
            P.op("act", lambda e, mt=mt, at=at: e.activation(at[:, 4, :], mt[:, 1, :], AF.Exp), reads=[mres], writes=[ares])
            P.op("dve", lambda e, mt=mt: e.tensor_tensor(mt[:, 1, :], mt[:, 1, :], mt[:, 0, :], ALU.subtract), reads=[mres], writes=[mres])
            P.op("act", lambda e, mt=mt: e.activation(mt[:, 1, :], mt[:, 1, :], AF.Exp), reads=[mres], writes=[mres])
            P.op("dve", lambda e, mt=mt, at=at: e.tensor_tensor(at[:, 3, :], mt[:, 1, :], at[:, 0, :], ALU.mult), reads=[mres, ares], writes=[ares])
            P.dma("sp" if n % 2 == 0 else "act", aux[sl, :].rearrange("t (a b) -> t a b", a=5), at[:], reads=[ares])
        P.flush()


def stage_ssd(P, nc, direction, xbcT, xtok, btok, aux, dsk, yT, consts_sb, rconst, T, Tp):
    NCH = Tp // 128
    ident_f = consts_sb[:, 0, :]
    tri = consts_sb[:, 1 + direction, :]
    maskS = consts_sb[:, 3 + direction, :]
    cbm = tri
    a0 = direction * 64
    with contextlib.ExitStack() as es:
        identb = sb(es, nc, "ss_idb", [128, 128], BF16)
        rib = Res()
        P.op("pool", lambda e: e.tensor_copy(identb[:], ident_f), reads=[rconst], writes=[rib])
        S32 = sb(es, nc, "ss_S32", [128, NG, 512], F32)
        Sbf = sb(es, nc, "ss_Sbf", [128, NG, 512], BF16)
        rS = [Res() for _ in range(NG)]
        rSb = [Res() for _ in range(NG)]
        dsb = sb(es, nc, "ss_d", [64, NH], F32)
        rd = Res()
        if direction == 0:
            P.dma("sp", dsb[:], dsk.partition_broadcast(64), writes=[rd])
        for g in range(NG):
            P.op("pool", lambda e, g=g: e.memset(S32[:, g, :], 0.0), writes=[rS[g]])
            P.op("pool", lambda e, g=g: e.memset(Sbf[:, g, :], 0.0), writes=[rSb[g]])
        auxr = Ring(es, nc, "ss_aux", 2, [128, 5, 128], F32)
        ctr = Ring(es, nc, "ss_ct", 2, [128, NG, 128], BF16)
        btr = Ring(es, nc, "ss_bt", 2, [128, NG, 128], BF16)
        bkr = Ring(es, nc, "ss_bk", 2, [128, GN], BF16)
        xkr = Ring(es, nc, "ss_xk", 2, [128, DI], BF16)
        cbr = Ring(es, nc, "ss_cb", 2, [128, 128], F32)
        dmr = Ring(es, nc, "ss_dm", 2, [128, 8, 128], F32)
        edr = Ring(es, nc, "ss_ed", 2, [128, 8, 128], F32)
        wr = Ring(es, nc, "ss_w", 2, [128, 8, 128], BF16)
        xdr = Ring(es, nc, "ss_xd", 2, [128, 512], BF16)
        xer = Ring(es, nc, "ss_xe", 2, [128, 512], BF16)
        zsr = Ring(es, nc, "ss_zs", 2, [128, 512], BF16)
        yor = Ring(es, nc, "ss_yo", 2, [64, 8, 128], F32)
        ypr = Ring(es, nc, "ss_yp", 2, [64, 8, 128], F32 if direction == 1 else BF16)
        ytr = Ring(es, nc, "ss_yt", 2, [64, 8, 128], F32)
        p_cb = Ring(es, nc, "ss_pcb", 1, [128, 512], F32, psum=True)
        p_df = Ring(es, nc, "ss_pdf", 1, [128, 1024], F32, psum=True)
        p_y = Ring(es, nc, "ss_py", 1, [64, 1024], F32, psum=True)
        p_z = Ring(es, nc, "ss_pz", 1, [128, 512], F32, psum=True)
        p_s = Ring(es, nc, "ss_pst", 1, [128, 512], F32, psum=True)
        order = list(range(NCH)) if direction == 0 else list(range(NCH - 1, -1, -1))
        for ci, c in enumerate(order):
            sl = slice(c * 128, (c + 1) * 128)
            ax, rax = auxr.next()
            ct, rct = ctr.next()
            bt, rbt = btr.next()
            bk, rbk = bkr.next()
            xk, rxk = xkr.next()
            P.dma("sp", ax[:], aux[sl, :].rearrange("t (a b) -> t a b", a=5), writes=[rax])
            P.dma("act", bt[:], xbcT[DI:DI + GN, sl].rearrange("(g n) t -> n g t", n=128), writes=[rbt])
            P.dma("sp", ct[:], xbcT[DI + GN:DI + 2 * GN, sl].rearrange("(g n) t -> n g t", n=128), writes=[rct])
            P.dma("act", bk[:], btok[sl, :], writes=[rbk])
            P.dma("sp", xk[:], xtok[sl, :], writes=[rxk])
            for g in range(NG):
                hs = slice(a0 + g * 8, a0 + g * 8 + 8)
                pcb, rpcb = p_cb.next()
                P.op("pe", lambda e, pcb=pcb, bt=bt, ct=ct, g=g: e.matmul(pcb[:, 0:128], bt[:, g, :], ct[:, g, :], start=True, stop=True),
                     reads=[rbt, rct], writes=[rpcb])
                cb, rcb = cbr.next()
                P.op("dve", lambda e, cb=cb, pcb=pcb: e.tensor_tensor(cb[:], pcb[:, 0:128], cbm, ALU.mult), reads=[rpcb, rconst], writes=[rcb])
                dm, rdm = dmr.next()
                P.op("dve", lambda e, dm=dm, ax=ax, hs=hs: e.tensor_tensor(
                    dm[:], maskS.unsqueeze(1).broadcast_to([128, 8, 128]), ax[:, 1, hs].unsqueeze(2).broadcast_to([128, 8, 128]), ALU.mult),
                    reads=[rax, rconst], writes=[rdm])
                pdf, rpdf = p_df.next()
                for h in range(8):
                    P.op("pe", lambda e, pdf=pdf, dm=dm, h=h: e.matmul(pdf[:, h * 128:(h + 1) * 128], dm[:, h, :], tri, start=True, stop=True),
                         reads=[rdm, rconst], writes=[rpdf])
                ed, red = edr.next()
                for hh in range(2):
                    P.op("act", lambda e, ed=ed, pdf=pdf, hh=hh: e.activation(
                        ed[:, 4 * hh:4 * hh + 4, :].rearrange("p a b -> p (a b)"), pdf[:, 512 * hh:512 * hh + 512], AF.Exp),
                        reads=[rpdf], writes=[red])
                w, rw = wr.next()
                P.op("dve", lambda e, w=w, ed=ed, cb=cb: e.tensor_tensor(w[:], ed[:], cb[:].unsqueeze(1).broadcast_to([128, 8, 128]), ALU.mult),
                     reads=[red, rcb], writes=[rw])
                xd, rxd = xdr.next()
                xe, rxe = xer.next()
                xg = xk[:, g * 512:(g + 1) * 512].rearrange("p (a b) -> p a b", a=8)
                P.op("dve", lambda e, xd=xd, xg=xg, ax=ax, hs=hs: e.tensor_tensor(
                    xd[:].rearrange("p (a b) -> p a b", a=8), xg, ax[:, 0, hs].unsqueeze(2).broadcast_to([128, 8, 64]), ALU.mult),
                    reads=[rxk, rax], writes=[rxd])
                P.op("dve", lambda e, xe=xe, xg=xg, ax=ax, hs=hs: e.tensor_tensor(
                    xe[:].rearrange("p (a b) -> p a b", a=8), xg, ax[:, 3, hs].unsqueeze(2).broadcast_to([128, 8, 64]), ALU.mult),
                    reads=[rxk, rax], writes=[rxe])
                first = (ci == 0)
                zs = None
                if not first:
                    pz, rpz = p_z.next()
                    P.op("pe", lambda e, pz=pz, ct=ct, g=g: e.matmul(pz[:], ct[:, g, :], Sbf[:, g, :], start=True, stop=True),
                         reads=[rct, rSb[g]], writes=[rpz])
                    zs, rzs = zsr.next()
                    P.op("dve", lambda e, zs=zs, pz=pz, ax=ax, hs=hs: e.tensor_tensor(
                        zs[:].rearrange("p (a b) -> p a b", a=8), pz[:].rearrange("p (a b) -> p a b", a=8),
                        ax[:, 2, hs].unsqueeze(2).broadcast_to([128, 8, 64]), ALU.mult), reads=[rpz, rax], writes=[rzs])
                py, rpy = p_y.next()
                for h in range(8):
                    P.op("pe", lambda e, py=py, xd=xd, w=w, h=h, first=first: e.matmul(
                        py[:, h * 128:(h + 1) * 128], xd[:, h * 64:(h + 1) * 64], w[:, h, :], start=True, stop=first),
                        reads=[rxd, rw], writes=[rpy])
                    if not first:
                        P.op("pe", lambda e, py=py, zs=zs, h=h: e.matmul(
                            py[:, h * 128:(h + 1) * 128], zs[:, h * 64:(h + 1) * 64], identb[:], start=False, stop=True),
                            reads=[rzs, rib], writes=[rpy])
                pst, rpst = p_s.next()
                P.op("pe", lambda e, pst=pst, bk=bk, xe=xe, g=g: e.matmul(pst[:], bk[:, g * 128:(g + 1) * 128], xe[:], start=True, stop=True),
                     reads=[rbk, rxe], writes=[rpst])
                P.op("dve", lambda e, g=g, ax=ax, hs=hs: e.tensor_tensor(
                    S32[:, g, :].rearrange("p (a b) -> p a b", a=8), S32[:, g, :].rearrange("p (a b) -> p a b", a=8),
                    ax[:, 4, hs].unsqueeze(2).broadcast_to([128, 8, 64]), ALU.mult), reads=[rax, rS[g]], writes=[rS[g]])
                P.op("dve", lambda e, g=g, pst=pst: e.tensor_tensor(S32[:, g, :], S32[:, g, :], pst[:], ALU.add),
                     reads=[rpst, rS[g]], writes=[rS[g]])
                P.op("act", lambda e, g=g: e.activation(Sbf[:, g, :], S32[:, g, :], AF.Copy), reads=[rS[g]], writes=[rSb[g]])
                ydst = yT[g * 512:(g + 1) * 512, sl].rearrange("(a p) t -> p a t", p=64)
                yp, ryp = ypr.next()
                yo, ryo = yor.next()
                py3 = py[:].rearrange("p (a b) -> p a b", a=8)
                if direction == 0:
                    P.dma("sp", yp[:], xbcT[g * 512:(g + 1) * 512, sl].rearrange("(a p) t -> p a t", p=64), writes=[ryp])
                    yt, ryt = ytr.next()
                    P.op("dve", lambda e, yt=yt, yp=yp, g=g: e.tensor_tensor(
                        yt[:], yp[:], dsb[:, g * 8:(g + 1) * 8].unsqueeze(2).broadcast_to([64, 8, 128]), ALU.mult),
                        reads=[ryp, rd], writes=[ryt])
                    for hh in range(2):
                        P.op("dve", lambda e, yo=yo, yt=yt, py3=py3, hh=hh: e.tensor_tensor(
                            yo[:, 4 * hh:4 * hh + 4, :], py3[:, 4 * hh:4 * hh + 4, :], yt[:, 4 * hh:4 * hh + 4, :], ALU.add),
                            reads=[rpy, ryt], writes=[ryo])
                else:
                    P.dma("sp", yp[:], ydst, writes=[ryp])
                    for hh in range(2):
                        P.op("dve", lambda e, yo=yo, yp=yp, py3=py3, hh=hh: e.tensor_tensor(
                            yo[:, 4 * hh:4 * hh + 4, :], py3[:, 4 * hh:4 * hh + 4, :], yp[:, 4 * hh:4 * hh + 4, :], ALU.add),
                            reads=[rpy, ryp], writes=[ryo])
                P.dma("act", ydst, yo[:], reads=[ryo])
        P.flush()


def stage_gnorm(P, nc, yT, proj, nw_sb, out, T):
    with contextlib.ExitStack() as es:
        onesG = sb(es, nc, "gn_ones", [128, 128], F32)
        r1 = Res()
        P.op("pool", lambda e: e.memset(onesG[:], 1.0 / 512.0), writes=[r1])
        yr = Ring(es, nc, "gn_y", 2, [128, 4, 512], F32)
        zr = Ring(es, nc, "gn_z", 2, [128, 4, 512], F32)
        sqr = Ring(es, nc, "gn_sq", 2, [128, 512], F32)
        rsr = Ring(es, nc, "gn_rs", 2, [128, 512], F32)
        ob = Ring(es, nc, "gn_o", 4, [128, 512], BF16)
        ps = Ring(es, nc, "gn_ps", 2, [128, 512], F32, psum=True)
        for (t0, tn) in tiles(T, 512):
            for g in range(NG):
                yt, ry = yr.next()
                zt, rz = zr.next()
                P.dma("sp", yt[:, :, :tn], yT[g * 512:(g + 1) * 512, t0:t0 + tn].rearrange("(kc p) t -> p kc t", p=128), writes=[ry])
                P.dma("act", zt[:, :, :tn], proj[g * 512:(g + 1) * 512, t0:t0 + tn].rearrange("(kc p) t -> p kc t", p=128), writes=[rz])
                P.op("act", lambda e, zt=zt, tn=tn: e.activation(zt[:, :, :tn], zt[:, :, :tn], AF.Silu), reads=[rz], writes=[rz])
                P.op("dve", lambda e, yt=yt, zt=zt, tn=tn: e.tensor_tensor(yt[:, :, :tn], yt[:, :, :tn], zt[:, :, :tn], ALU.mult), reads=[ry, rz], writes=[ry])
                pt, pres = ps.next()
                for kc in range(4):
                    st, rs_ = sqr.next()
                    P.op("act", lambda e, st=st, yt=yt, kc=kc, tn=tn: e.activation(st[:, :tn], yt[:, kc, :tn], AF.Square), reads=[ry], writes=[rs_])
                    P.op("pe", lambda e, pt=pt, st=st, kc=kc, tn=tn: e.matmul(pt[:, :tn], onesG[:], st[:, :tn], start=(kc == 0), stop=(kc == 3)),
                         reads=[rs_, r1], writes=[pres])
                rt, rr = rsr.next()
                P.op("act", lambda e, rt=rt, pt=pt, tn=tn: e.activation(rt[:, :tn], pt[:, :tn], AF.Ln, bias=EPS, scale=1.0), reads=[pres], writes=[rr])
                P.op("act", lambda e, rt=rt, tn=tn: e.activation(rt[:, :tn], rt[:, :tn], AF.Exp, scale=-0.5), reads=[rr], writes=[rr])
                for kc in range(4):
                    ot, ro = ob.next()
                    P.op("dve", lambda e, ot=ot, yt=yt, kc=kc, tn=tn, rt=rt, g=g: e.scalar_tensor_tensor(
                        ot[:, :tn], yt[:, kc, :tn], nw_sb[:, g * 4 + kc:g * 4 + kc + 1], rt[:, :tn], ALU.mult, ALU.mult),
                        reads=[ry, rr], writes=[ro])
                    P.dma("sp" if kc % 2 == 0 else "act", out[(g * 4 + kc) * 128:(g * 4 + kc + 1) * 128, t0:t0 + tn], ot[:, :tn], reads=[ro])
        P.flush()


def build_program(seq, depth):
    T = seq + NMETA
    Tp = ((T + 127) // 128) * 128
    NCH = Tp // 128
    KT = len(tiles(T, 256))
    n_f = (depth + 1) // 2
    n_s = depth // 2
    nc = bass.Bass("TRN2", target_bir_lowering=False)
    dt = nc.dram_tensor

    def ext(name, shape, dtype=F32):
        return dt(name, shape, dtype, kind="ExternalInput").ap()

    x_in = ext("x_in", [D, T])
    consts = ext("consts", [128, 6, 128])
    normw = ext("normw", [128, (2 * depth + 1) * 16])
    wg = [ext("wg%d" % i, [DFF // 128, 128, D]) for i in range(depth)]
    wu = [ext("wu%d" % i, [DFF // 128, 128, D]) for i in range(depth)]
    wd = [ext("wd%d" % i, [D // 128, 128, DFF]) for i in range(depth)]
    wf = [ext("wf%d" % i, [D // 128, 128, D]) for i in range(n_f)]
    if n_f:
        csc = ext("csc", [256, 512], BF16)
        tabc = ext("tabc", [KT, 128, NCH * 256], BF16)
        tabs = ext("tabs", [KT, 128, NCH * 256], BF16)
    win = [ext("win%d" % i, [DPROJ // 128, 128, D]) for i in range(n_s)]
    wout = [ext("wout%d" % i, [D // 128, 128, DI]) for i in range(n_s)]
    cw = [ext("cw%d" % i, [128, 48, KW]) for i in range(n_s)]
    cbi = [ext("cb%d" % i, [128, 48]) for i in range(n_s)]
    dtb = [ext("dtb%d" % i, [128, 64]) for i in range(n_s)]
    alog = [None for i in range(n_s)]
    dsk = [ext("dsk%d" % i, [1, NH]) for i in range(n_s)]
    gnw = [ext("gnw%d" % i, [128, 32]) for i in range(n_s)]
    y_out = dt("y_out", [D, T], F32, kind="ExternalOutput").ap()

    res = dt("res", [D, T], F32).ap()
    uT = dt("uT", [D, Tp], BF16).ap()
    hT = dt("hT", [DFF, T], BF16).ap()
    if n_f:
        pq = dt("pq", [Tp, 4096], BF16).ap()
        rT = dt("rT", [D, T], BF16).ap()
    if n_s:
        proj = dt("proj", [DPROJ, T], F32).ap()
        xbcT = dt("xbcT", [CONVD, Tp], BF16).ap()
        xtok = dt("xtok", [Tp, DI], BF16).ap()
        btok = dt("btok", [Tp, GN], BF16).ap()
        aux = dt("aux", [Tp, 5 * 128], F32).ap()
        yT = dt("yT", [DI, Tp], F32).ap()
        gT = dt("gT", [DI, T], BF16).ap()

    with contextlib.ExitStack() as es:
        P = Prog(nc, es)
        consts_sb = sb(es, nc, "consts_sb", [128, 6, 128], F32)
        normw_sb = sb(es, nc, "normw_sb", [128, (2 * depth + 1) * 16], F32)
        gnw_sb = [sb(es, nc, "gnw_sb%d" % i, [128, 32], F32) for i in range(n_s)]
        P.dma("sp", consts_sb[:], consts)
        P.dma("sp", normw_sb[:], normw)
        for i in range(n_s):
            P.dma("sp", gnw_sb[i][:], gnw[i])
        P.flush()
        rconst = Res()
        stage_copy_in(P, nc, res, x_in, T)
        stage_zero_pad(P, nc, [uT], T, Tp)
        for i in range(depth):
            j = i // 2
            stage_rmsnorm(P, nc, res, normw_sb, (2 * i) * 16, uT, T, consts_sb)
            if i % 2 == 0:
                stage_fnet_chan(P, nc, uT, csc, pq, Tp)
                stage_fnet_pos(P, nc, pq, tabc, tabs, rT, T, Tp)
                s_, e_ = epi_resadd(P, nc, res)
                stage_gemm(P, nc, "gf", rT, D, T, [wf[j]], D, min(T, 4112), s_, e_)
            else:
                KS = int(os.environ.get("KSTOP", "99"))
                s_, e_ = epi_store(P, nc, proj, F32)
                stage_gemm(P, nc, "gi", uT, D, T, [win[j]], DPROJ, min(T, 4112), s_, e_)
                if KS >= 2:
                    stage_conv(P, nc, proj, cw[j], cbi[j], xbcT, xtok, btok, consts_sb, rconst, T, Tp)
                if KS >= 3:
                    stage_dt(P, nc, proj, dtb[j], alog[j], aux, consts_sb, rconst, T, Tp)
                if KS >= 4:
                    stage_ssd(P, nc, 0, xbcT, xtok, btok, aux, dsk[j], yT, consts_sb, rconst, T, Tp)
                if KS >= 5:
                    stage_ssd(P, nc, 1, xbcT, xtok, btok, aux, dsk[j], yT, consts_sb, rconst, T, Tp)
                if KS >= 6:
                    stage_gnorm(P, nc, yT, proj, gnw_sb[j], gT, T)
                if KS >= 7:
                    s_, e_ = epi_resadd(P, nc, res)
                    stage_gemm(P, nc, "go", gT, DI, T, [wout[j]], D, min(T, 2056), s_, e_)
            stage_rmsnorm(P, nc, res, normw_sb, (2 * i + 1) * 16, uT, T, consts_sb)
            s_, e_ = epi_swiglu(P, nc, hT)
            stage_gemm(P, nc, "g1", uT, D, T, [wg[i], wu[i]], DFF, min(T, 4112), s_, e_)
            s_, e_ = epi_resadd(P, nc, res)
            stage_gemm(P, nc, "g2", hT, DFF, T, [wd[i]], D, min(T, 1372), s_, e_)
        stage_rmsnorm(P, nc, res, normw_sb, (2 * depth) * 16, y_out, T, consts_sb, out_f32=True)
    return nc


def tile_major(w):
    K, M = w.shape
    return np.ascontiguousarray(w.reshape(K // 128, 128, M // 128, 128).transpose(2, 1, 0, 3).reshape(M // 128, 128, K))


def make_consts():
    k = np.arange(128)[:, None]
    l = np.arange(128)[None, :]
    c = np.zeros((128, 6, 128), np.float32)
    c[:, 0] = (k == l)
    c[:, 1] = (k <= l)
    c[:, 2] = (k >= l)
    c[:, 3] = (k > l)
    c[:, 4] = (k < l)
    c[:, 5] = 1.0
    return c


def make_tables(T, Tp):
    NCH = Tp // 128
    n = np.arange(T, dtype=np.int64)
    m = (n[:, None] * n[None, :]) % T
    ang = (2.0 * np.pi / T) * m.astype(np.float64)
    kt = tiles(T, 256)
    outs = []
    for fn, sgn in ((np.cos, 1.0), (np.sin, -1.0)):
        full = np.zeros((Tp, len(kt) * 256), np.float32)
        full[:T, :T] = (sgn * fn(ang)).astype(np.float32)
        t = full.reshape(NCH, 128, len(kt), 256).transpose(2, 1, 0, 3).reshape(len(kt), 128, NCH * 256)
        outs.append(np.ascontiguousarray(t).astype(ml_dtypes.bfloat16))
    c = np.arange(256, dtype=np.int64)
    a2 = (2.0 * np.pi / 256.0) * ((c[:, None] * c[None, :]) % 256).astype(np.float64)
    csc = np.concatenate([np.cos(a2), np.sin(a2)], axis=1).astype(np.float32).astype(ml_dtypes.bfloat16)
    return outs[0], outs[1], csc


_CACHE = {}


def run_module(x, meta_tokens, norm_mix_w, norm_ffn_w, norm_final_w, fnet_w_out, ssd_w_in, ssd_conv_w,
               ssd_conv_b, ssd_dt_bias, ssd_a_log, ssd_d, ssd_norm_w, ssd_w_out, ffn_w_gate, ffn_w_up,
               ffn_w_down, depth=None):
    x = np.asarray(x, np.float32)
    B, seq, _ = x.shape
    depth = int(np.asarray(norm_mix_w).shape[0]) if depth is None else depth
    T = seq + NMETA
    Tp = ((T + 127) // 128) * 128
    n_f = (depth + 1) // 2
    n_s = depth // 2
    key = (seq, depth)
    if key not in _CACHE:
        _CACHE[key] = build_program(seq, depth)
    nc = _CACHE[key]
    f = lambda a: np.asarray(a, np.float32)
    col = lambda v: np.ascontiguousarray(f(v).reshape(-1, 128).T)
    shared = {"consts": make_consts()}
    nw = []
    for i in range(depth):
        nw.append(col(f(norm_mix_w)[i]))
        nw.append(col(f(norm_ffn_w)[i]))
    nw.append(col(f(norm_final_w)))
    shared["normw"] = np.ascontiguousarray(np.concatenate(nw, axis=1))
    for i in range(depth):
        shared["wg%d" % i] = tile_major(f(ffn_w_gate)[i])
        shared["wu%d" % i] = tile_major(f(ffn_w_up)[i])
        shared["wd%d" % i] = tile_major(f(ffn_w_down)[i])
    if n_f:
        tc, ts, csc = make_tables(T, Tp)
        shared["tabc"], shared["tabs"], shared["csc"] = tc, ts, csc
    for i in range(n_f):
        shared["wf%d" % i] = tile_major(f(fnet_w_out)[i])
    for i in range(n_s):
        shared["win%d" % i] = tile_major(f(ssd_w_in)[i])
        shared["wout%d" % i] = tile_major(f(ssd_w_out)[i])
        cwi = f(ssd_conv_w)[i]
        shared["cw%d" % i] = np.ascontiguousarray(cwi.T.reshape(48, 128, KW).transpose(1, 0, 2))
        shared["cb%d" % i] = col(f(ssd_conv_b)[i])
        dtp = np.zeros((128, 64), np.float32)
        dtp[:, 0] = f(ssd_dt_bias)[i].reshape(128)
        dtp[:, 1] = f(ssd_a_log)[i].reshape(128)
        shared["dtb%d" % i] = dtp
        shared["dsk%d" % i] = np.ascontiguousarray(f(ssd_d)[i].reshape(1, NH))
        shared["gnw%d" % i] = col(f(ssd_norm_w)[i])
    in_maps = []
    for b in range(B):
        h0 = np.concatenate([f(meta_tokens), x[b]], axis=0)
        m = dict(shared)
        m["x_in"] = np.ascontiguousarray(h0.T)
        in_maps.append(m)
    r = run_bass_kernel_spmd(nc, in_maps, core_ids=list(range(B)))
    out = np.stack([np.ascontiguousarray(r.results[b]["y_out"].T[NMETA:]) for b in range(B)], axis=0)
    return out.astype(np.float32)


def kernel(**inputs):
    return run_module(**inputs)
```
